# Optimizing a Trainium2 kernel written in Bass

```python
import jax, jax.numpy as jnp
from jax import lax
import numpy as np

D_MODEL = 2048
BATCH = 4
SEQ = 2048
DEPTH = 2

CTX_LEN = 256
GRID_W = 64
HEAD_DIM = 128
ATT_W = (3 * D_MODEL) // 4
N_Q_HEADS = ATT_W // HEAD_DIM
N_KV_HEADS = 4
Q_PER_KV = N_Q_HEADS // N_KV_HEADS
KV_W = N_KV_HEADS * HEAD_DIM
FNET_W = D_MODEL // 4
N_FNET_GROUPS = 4
FNET_GROUP = FNET_W // N_FNET_GROUPS
ROPE_THETA = 10000.0
ROPE_AXIS_DIM = HEAD_DIM // 2
ROPE_NFREQ = ROPE_AXIS_DIM // 2
Q_BLOCK = 128
EPS = 1e-6

OFF_Q = 0
OFF_K = OFF_Q + ATT_W
OFF_V = OFF_K + KV_W
OFF_ZA = OFF_V + KV_W
OFF_UB = OFF_ZA + ATT_W
OFF_ZB = OFF_UB + FNET_W
OFF_GA = OFF_ZB + FNET_W
OFF_GB = OFF_GA + D_MODEL
IN_W = OFF_GB + D_MODEL
R_ZA = 0
R_UB = OFF_UB - OFF_ZA
R_ZB = OFF_ZB - OFF_ZA
R_GA = OFF_GA - OFF_ZA
R_GB = OFF_GB - OFF_ZA

kernel_name = "hybrid_gqa_fnet_gated_dit_block"


def _rmsnorm(x, g):
    xf = x.astype(jnp.float32)
    y = xf * lax.rsqrt(jnp.mean(xf * xf, axis=-1, keepdims=True) + EPS)
    return (y * g.astype(jnp.float32)).astype(x.dtype)


def _modulate(h, shift, scale):
    return h * (1 + scale) + shift


def _heads(p, lo, n_heads):
    b, n, _ = p.shape
    return p[..., lo:lo + n_heads * HEAD_DIM].reshape(b, n, n_heads, HEAD_DIM)


def _axial_rope_tables(n_tokens):
    rows = n_tokens // GRID_W
    row_pos = jnp.repeat(jnp.arange(rows, dtype=jnp.float32), GRID_W)
    col_pos = jnp.tile(jnp.arange(GRID_W, dtype=jnp.float32), rows)
    inv = ROPE_THETA ** (-jnp.arange(ROPE_NFREQ, dtype=jnp.float32) / ROPE_NFREQ)
    ang = jnp.stack([row_pos[:, None] * inv, col_pos[:, None] * inv], axis=1)
    return jnp.cos(ang), jnp.sin(ang)


def _rope(x, cos, sin):
    b, n, h, _ = x.shape
    xf = x.astype(jnp.float32).reshape(b, n, h, 2, 2, ROPE_NFREQ)
    x1, x2 = xf[..., 0, :], xf[..., 1, :]
    cc, ss = cos[None, :, None], sin[None, :, None]
    out = jnp.stack([x1 * cc - x2 * ss, x2 * cc + x1 * ss], axis=-2)
    return out.reshape(b, n, h, HEAD_DIM).astype(x.dtype)


def _attention(q, k, v):
    b, n = q.shape[:2]
    nblk = n // Q_BLOCK
    qb = q.reshape(b, nblk, Q_BLOCK, N_KV_HEADS, Q_PER_KV, HEAD_DIM).transpose(1, 0, 2, 3, 4, 5)

    def block(qi):
        s = jnp.einsum('bqhgd,bkhd->bhgqk', qi, k, preferred_element_type=jnp.float32)
        w = jax.nn.softmax(s, axis=-1).astype(v.dtype)
        return jnp.einsum('bhgqk,bkhd->bqhgd', w, v)

    o = lax.map(block, qb)
    return o.transpose(1, 0, 2, 3, 4, 5).reshape(b, n, ATT_W)


def _fourier_mix(u):
    b, n, _ = u.shape
    ug = u.astype(jnp.float32).reshape(b, n, N_FNET_GROUPS, FNET_GROUP)
    y = jnp.fft.fftn(ug, axes=(1, 3), norm="ortho").real
    return y.reshape(b, n, FNET_W).astype(u.dtype)


def _merge(p_rest, attn, w_pa, w_pb, w_o):
    z_a = p_rest[..., R_ZA:R_UB]
    u_b = p_rest[..., R_UB:R_ZB]
    z_b = p_rest[..., R_ZB:R_GA]
    g_a = jax.nn.sigmoid(p_rest[..., R_GA:R_GB])
    g_b = jax.nn.sigmoid(p_rest[..., R_GB:])
    y_a = (attn * jax.nn.silu(z_a)) @ w_pa
    y_b = (_fourier_mix(u_b) * jax.nn.silu(z_b)) @ w_pb
    return (g_a * y_a + g_b * y_b) @ w_o


def setup_inputs(seed: int = 0) -> dict:
    key = jax.random.key(seed)
    ks = jax.random.split(key, 16)
    f32 = jnp.float32
    nrm = lambda k, shape, s: jax.random.normal(k, shape, f32) * s
    return {
        "x": nrm(ks[0], (BATCH, SEQ, D_MODEL), 1.0),
        "c": nrm(ks[1], (BATCH, D_MODEL), 1.0),
        "ctx": nrm(ks[2], (BATCH, CTX_LEN, D_MODEL), 1.0),
        "c_ctx": nrm(ks[3], (D_MODEL,), 1.0),
        "w_ada": nrm(ks[4], (DEPTH, D_MODEL, 3 * D_MODEL), D_MODEL ** -0.5),
        "b_ada": nrm(ks[5], (DEPTH, 3 * D_MODEL), 0.01),
        "norm_g": 1.0 + nrm(ks[6], (DEPTH, D_MODEL), 0.01),
        "w_in": nrm(ks[7], (DEPTH, D_MODEL, IN_W), D_MODEL ** -0.5),
        "q_norm_g": 1.0 + nrm(ks[8], (DEPTH, HEAD_DIM), 0.01),
        "k_norm_g": 1.0 + nrm(ks[9], (DEPTH, HEAD_DIM), 0.01),
        "w_proj_a": nrm(ks[10], (DEPTH, ATT_W, D_MODEL), ATT_W ** -0.5),
        "w_proj_b": nrm(ks[11], (DEPTH, FNET_W, D_MODEL), FNET_W ** -0.5),
        "w_out": nrm(ks[12], (DEPTH, D_MODEL, D_MODEL), D_MODEL ** -0.5),
        "final_g": 1.0 + nrm(ks[13], (D_MODEL,), 0.01),
    }


def reference(x, c, ctx, c_ctx, w_ada, b_ada, norm_g, w_in, q_norm_g, k_norm_g,
              w_proj_a, w_proj_b, w_out, final_g):
    n_lat = x.shape[1]
    cos, sin = _axial_rope_tables(n_lat)
    q_scale = HEAD_DIM ** -0.5
    silu_c = jax.nn.silu(c)
    silu_cc = jax.nn.silu(c_ctx)
    xs, cs = x, ctx
    for l in range(DEPTH):
        last = l == DEPTH - 1
        w_l = w_in[l]
        shift, scale, gate = jnp.split(silu_c @ w_ada[l] + b_ada[l], 3, axis=-1)
        shift_c, scale_c, gate_c = jnp.split(silu_cc @ w_ada[l] + b_ada[l], 3, axis=-1)
        h = _modulate(_rmsnorm(xs, norm_g[l]), shift[:, None], scale[:, None])
        hc = _modulate(_rmsnorm(cs, norm_g[l]), shift_c, scale_c)

        pc_kv = hc @ w_l[:, OFF_K:OFF_ZA]
        kc = _rmsnorm(_heads(pc_kv, 0, N_KV_HEADS), k_norm_g[l])
        vc = _heads(pc_kv, KV_W, N_KV_HEADS)

        p = h @ w_l
        q = _rope(_rmsnorm(_heads(p, OFF_Q, N_Q_HEADS), q_norm_g[l]), cos, sin) * q_scale
        k = _rope(_rmsnorm(_heads(p, OFF_K, N_KV_HEADS), k_norm_g[l]), cos, sin)
        v = _heads(p, OFF_V, N_KV_HEADS)
        attn = _attention(q, jnp.concatenate([k, kc], axis=1), jnp.concatenate([v, vc], axis=1))
        y = _merge(p[..., OFF_ZA:], attn, w_proj_a[l], w_proj_b[l], w_out[l])
        new_xs = xs + gate[:, None] * y

        if not last:
            qc = _rmsnorm(_heads(hc @ w_l[:, OFF_Q:OFF_K], 0, N_Q_HEADS), q_norm_g[l]) * q_scale
            attn_c = _attention(qc, kc, vc)
            yc = _merge(hc @ w_l[:, OFF_ZA:], attn_c, w_proj_a[l], w_proj_b[l], w_out[l])
            cs = cs + gate_c * yc
        xs = new_xs
    return _rmsnorm(xs, final_g)
```

```python
import numpy as np
import ml_dtypes
import concourse.bass as bass
import concourse.mybir as mybir
from concourse.bass_utils import run_bass_kernel_spmd

F32 = mybir.dt.float32
BF16 = mybir.dt.bfloat16
AF = mybir.ActivationFunctionType
ALU = mybir.AluOpType
EPS = 1e-6
GRID_W = 64
ROPE_THETA = 10000.0


class T:
    __slots__ = ("t", "lw", "rd", "name")

    def __init__(self, t, name=""):
        self.t = t
        self.lw = {}
        self.rd = []
        self.name = name

    def __getitem__(self, idx):
        return self.t[idx]


class FW:
    NDMA = 6

    def __init__(self, nc):
        self.nc = nc
        self.eng = {"pe": nc.tensor, "act": nc.scalar, "dve": nc.vector,
                    "pool": nc.gpsimd, "sp": nc.sync}
        self.sem = {}
        self.cnt = {}
        for k in self.eng:
            self.sem[k] = nc.alloc_semaphore("s_" + k)
            self.cnt[k] = 0
        self.dsem = {}
        self.dcnt = {}
        self.dnext = {}
        for q in ("sp", "pool"):
            self.dsem[q] = [nc.alloc_semaphore("d_%s%d" % (q, i)) for i in range(self.NDMA)]
            self.dcnt[q] = [0] * self.NDMA
            self.dnext[q] = 0
        self.waited = {}
        self.ninstr = 0

    def sb(self, name, shape, dt):
        return T(self.nc.alloc_sbuf_tensor(name, list(shape), dt), name)

    def ps(self, name, shape, dt=F32):
        return T(self.nc.alloc_psum_tensor(name, list(shape), dt), name)

    def semof(self, key):
        if isinstance(key, tuple):
            return self.dsem[key[0]][key[1]]
        return self.sem[key]

    def _wait(self, e, ev):
        if ev is None:
            return
        key, val = ev
        if self.waited.get((e, key), 0) >= val:
            return
        self.waited[(e, key)] = val
        self.eng[e].wait_ge(self.semof(key), val)

    def _deps(self, e, reads, writes):
        for t in reads:
            for k, v in t.lw.items():
                if k == e and e == "pe":
                    continue
                self._wait(e, (k, v))
        for t in writes:
            for k, v in t.lw.items():
                if k != e or e != "pe":
                    self._wait(e, (k, v))
            for ev in t.rd:
                if ev[0] != e or e != "pe":
                    self._wait(e, ev)

    def _commit(self, ev, reads, writes):
        for t in reads:
            t.rd.append(ev)
            if len(t.rd) > 48:
                m = {}
                for k, v in t.rd:
                    if m.get(k, 0) < v:
                        m[k] = v
                t.rd = list(m.items())
        for t in writes:
            if t.lw.get(ev[0], 0) < ev[1]:
                t.lw[ev[0]] = ev[1]
            t.rd = []

    def op(self, e, fn, reads=(), writes=()):
        self._deps(e, reads, writes)
        ins = fn(self.eng[e])
        self.cnt[e] += 1
        ins.then_inc(self.sem[e], 1)
        ev = (e, self.cnt[e])
        self._commit(ev, reads, writes)
        self.ninstr += 1
        return ev

    def dma(self, q, out, in_, reads=(), writes=(), **kw):
        slot = self.dnext[q]
        self.dnext[q] = (slot + 1) % self.NDMA
        key = (q, slot)
        if self.dcnt[q][slot] > 0:
            self._wait(q, (key, self.dcnt[q][slot]))
        self._deps(q, reads, writes)
        ins = self.eng[q].dma_start(out=out, in_=in_, **kw)
        self.dcnt[q][slot] += 16
        ins.then_inc(self.dsem[q][slot], 16)
        ev = (key, self.dcnt[q][slot])
        self._commit(ev, reads, writes)
        self.ninstr += 1
        return ev

    def handoff(self, new_tiles, old_tiles):
        m = {}
        for t in old_tiles:
            evs = list(t.rd) + list(t.lw.items())
            for k, v in evs:
                if m.get(k, 0) < v:
                    m[k] = v
        for t in new_tiles:
            mm = dict(m)
            for k, v in t.rd + list(t.lw.items()):
                if mm.get(k, 0) < v:
                    mm[k] = v
            t.lw = {}
            t.rd = list(mm.items())

    def finish(self, tiles):
        for t in tiles:
            for k, v in t.lw.items():
                self._wait("sp", (k, v))


class Cfg:
    def __init__(self, D=2048, N=2048, C=256, TB=512, DEPTH=2, NOWN=1024):
        self.D, self.N, self.C, self.TB, self.DEPTH, self.NOWN = D, N, C, TB, DEPTH, NOWN
        self.KC = D // 128
        self.ATT_W = 3 * D // 4
        self.NQ = self.ATT_W // 128
        self.NKV = self.NQ // 3
        self.KV_W = self.NKV * 128
        self.FW_ = D // 4
        self.NG = self.FW_ // 128
        self.OFF_Q = 0
        self.OFF_K = self.ATT_W
        self.OFF_V = self.OFF_K + self.KV_W
        self.OFF_ZA = self.OFF_V + self.KV_W
        self.OFF_UB = self.OFF_ZA + self.ATT_W
        self.OFF_ZB = self.OFF_UB + self.FW_
        self.OFF_GA = self.OFF_ZB + self.FW_
        self.OFF_GB = self.OFF_GA + D
        self.IN_W = self.OFF_GB + D
        self.NT = N + C
        self.NTT = self.NT // 128
        self.NLT = N // 128


def build_program(cfg):
    D, N, C, TB, KC, NT = cfg.D, cfg.N, cfg.C, cfg.TB, cfg.KC, cfg.NT
    NQ, NKV, NG, DEPTH = cfg.NQ, cfg.NKV, cfg.NG, cfg.DEPTH
    CB = min(TB, C)
    nc = bass.Bass("TRN2", target_bir_lowering=False)
    fw = FW(nc)

    def din(name, shape, dt=F32):
        return T(nc.dram_tensor(name, list(shape), dt, kind="ExternalInput").ap(), name)

    def dscr(name, shape, dt):
        return T(nc.dram_tensor(name, list(shape), dt, kind="Internal").ap(), name)

    x_d = din("x", [N, D])
    ctx_d = din("ctx", [C, D])
    c_d = din("c", [D])
    cc_d = din("c_ctx", [D])
    wada_d = din("w_ada", [DEPTH, D, 3 * D])
    bada_d = din("b_ada", [DEPTH, 3 * D])
    ng_d = din("norm_g", [DEPTH, D])
    win_d = din("w_in", [DEPTH, D, cfg.IN_W])
    qg_d = din("q_norm_g", [DEPTH, 128])
    kg_d = din("k_norm_g", [DEPTH, 128])
    wpa_d = din("w_proj_a", [DEPTH, cfg.ATT_W, D])
    wpb_d = din("w_proj_b", [DEPTH, cfg.FW_, D])
    wo_d = din("w_out", [DEPTH, D, D])
    fg_d = din("final_g", [D])
    cos_d = din("t_cos", [128, N])
    sin_d = din("t_sin", [128, N])
    rt_d = din("t_rt", [128, 128])
    id_d = din("t_id", [128, 128])
    csc_d = din("t_csc", [128, 256], BF16)
    dftl_d = din("t_dftl", [N // 128, N, 256], BF16)
    dftc_d = din("t_dftc", [C // 128, C, 256], BF16)
    out_d = T(nc.dram_tensor("out", [cfg.NOWN, D], F32, kind="ExternalOutput").ap(), "out")

    xs_d = dscr("xs_scr", [NT, D], F32)
    gt_d = dscr("gt_scr", [KC, 128, NT], BF16)
    mt_d = dscr("mt_scr", [KC, 128, NT], BF16)

    OB = min(512, TB)
    B1 = max(KC * NT, 6 * D + KC * 128)
    B2 = KC * max(NT, D)
    big1 = nc.alloc_sbuf_tensor("big1", [128, B1], BF16)
    big2 = nc.alloc_sbuf_tensor("big2", [128, B2], BF16)
    big1f = big1.bitcast(F32)
    hT = T(big1, "hT")
    gbl = T(big1f, "gbl")
    gbc = T(big1f, "gbc")
    fgb = T(big1f, "fgb")
    mtt = T(big1, "mtt")
    MT0 = 6 * D
    KT = T(big2, "KT")
    Vt = T(big2, "V")
    AB = T(big2, "AB")
    GT = T(big2, "GT")
    WO = T(big2, "WO")
    KT0, V0, AB0 = 0, NKV * NT, 2 * NKV * NT
    xt = fw.sb("xt", [128, D], F32)
    junk = fw.sb("junk", [128, D], BF16)
    wr = [fw.sb("wr%d" % i, [128, KC, 256], BF16) for i in range(3)]
    ftmp = {n: fw.sb("f_" + n, [128, TB], F32) for n in ("sq", "rs", "qn", "t1", "t2", "cosb", "sinb")}
    qT = fw.sb("qT", [128, NT], BF16)
    pT = [fw.sb("pT%d" % i, [128, TB], BF16) for i in range(3)]
    uT = fw.sb("uT", [128, TB], BF16)
    ident = fw.sb("ident", [128, 128], F32)
    rT = fw.sb("rT", [128, 128], F32)
    dg = fw.sb("dg", [128, 128], F32)
    ones32 = fw.sb("ones32", [128, 128], F32)
    onesb = fw.sb("onesb", [128, 128], BF16)
    csc = fw.sb("csc", [128, 256], BF16)
    epsc = fw.sb("epsc", [128, 1], F32)
    cT = fw.sb("cT", [128, KC, 2], F32)
    scT = fw.sb("scT", [128, KC, 2], BF16)
    gcol = fw.sb("gcol", [128, KC], F32)
    bcol = fw.sb("bcol", [128, 3 * KC], F32)
    modc = [fw.sb("modc%d" % s, [128, 3 * KC], F32) for s in range(2)]
    gsc = [fw.sb("gsc%d" % s, [128, KC], F32) for s in range(2)]
    qgc = fw.sb("qgc", [128, 1], F32)
    kgc = fw.sb("kgc", [128, 1], F32)
    ssq = fw.sb("ssq", [128, 1], F32)
    rstd = fw.sb("rstd", [128, 1], F32)
    banks = [fw.ps("bank%d" % i, [128, 512]) for i in range(8)]
    bstate = {"i": 0}

    def bank():
        b = banks[bstate["i"] % 8]
        bstate["i"] += 1
        return b

    fw.dma("sp", ident[:], id_d[:], reads=[id_d], writes=[ident])
    fw.dma("sp", rT[:], rt_d[:], reads=[rt_d], writes=[rT])
    fw.dma("sp", csc[:], csc_d[:], reads=[csc_d], writes=[csc])
    fw.op("dve", lambda e: e.memset(ones32[:], 1.0), writes=[ones32])
    fw.op("dve", lambda e: e.memset(onesb[:], 1.0), writes=[onesb])
    fw.op("dve", lambda e: e.memset(epsc[:], EPS), writes=[epsc])
    with nc.allow_non_contiguous_dma(reason="tiny column-layout loads"):
        fw.dma("sp", cT[:, :, 0], c_d[:].rearrange("(k p) -> p k", p=128), reads=[c_d], writes=[cT])
        fw.dma("sp", cT[:, :, 1], cc_d[:].rearrange("(k p) -> p k", p=128), reads=[cc_d], writes=[cT])
    fw.op("act", lambda e: e.activation(out=scT[:], in_=cT[:], func=AF.Silu), reads=[cT], writes=[scT])

    lat_blocks = [(s, TB, 0) for s in range(0, N, TB)]
    CB = min(TB, C)
    ctx_blocks = [(N + s, CB, 1) for s in range(0, C, CB)]
    wstate = {"i": 0}

    def wslot():
        w = wr[wstate["i"] % 3]
        wstate["i"] += 1
        return w

    def load_w(dram_t, ap_rows_cols, slot, k0=0):
        rows, cols = ap_rows_cols.shape
        fw.dma("pool", slot[:, k0:k0 + rows // 128, 0:cols],
               ap_rows_cols.rearrange("(t p) c -> p t c", p=128), reads=[dram_t], writes=[slot])

    def run_jobs(jobs):
        slots = [None] * len(jobs)
        if jobs:
            slots[0] = wslot()
            jobs[0][0](slots[0])
        for i, (ld, cp) in enumerate(jobs):
            if i + 1 < len(jobs):
                slots[i + 1] = wslot()
                jobs[i + 1][0](slots[i + 1])
            cp(slots[i])

    def proj_block(slot, cb, blk, kc=KC):
        s0, sz, _ = blk
        pb = bank()
        for k in range(kc):
            fw.op("pe", lambda e, k=k: e.matmul(pb[:, 0:sz], slot[:, k, cb * 128:(cb + 1) * 128],
                                                 hT[:, k * NT + s0:k * NT + s0 + sz],
                                                 start=(k == 0), stop=(k == kc - 1)),
                  reads=[slot, hT], writes=[pb])
        return pb

    def headnorm(pb, sz, gc, dst_ap, dst_t, rope_blk=None):
        sq, rs, qn, t1, t2 = ftmp["sq"], ftmp["rs"], ftmp["qn"], ftmp["t1"], ftmp["t2"]
        fw.op("act", lambda e: e.activation(out=sq[:, 0:sz], in_=pb[:, 0:sz], func=AF.Square), reads=[pb], writes=[sq])
        p2 = bank()
        fw.op("pe", lambda e: e.matmul(p2[:, 0:sz], ones32[:], sq[:, 0:sz], start=True, stop=True),
              reads=[ones32, sq], writes=[p2])
        fw.op("act", lambda e: e.activation(out=rs[:, 0:sz], in_=p2[:, 0:sz], func=AF.Sqrt, scale=1.0 / 128,
                                            bias=epsc[:, 0:1]), reads=[p2, epsc], writes=[rs])
        fw.op("dve", lambda e: e.reciprocal(out=rs[:, 0:sz], in_=rs[:, 0:sz]), reads=[rs], writes=[rs])
        if rope_blk is None:
            fw.op("dve", lambda e: e.scalar_tensor_tensor(out=dst_ap, in0=pb[:, 0:sz], scalar=gc[:, 0:1],
                                                          in1=rs[:, 0:sz], op0=ALU.mult, op1=ALU.mult),
                  reads=[pb, gc, rs], writes=[dst_t])
            return
        s0 = rope_blk
        cosb, sinb = ftmp["cosb"], ftmp["sinb"]
        fw.dma("sp", cosb[:, 0:sz], cos_d[:, s0:s0 + sz], reads=[cos_d], writes=[cosb])
        fw.dma("sp", sinb[:, 0:sz], sin_d[:, s0:s0 + sz], reads=[sin_d], writes=[sinb])
        fw.op("dve", lambda e: e.scalar_tensor_tensor(out=qn[:, 0:sz], in0=pb[:, 0:sz], scalar=gc[:, 0:1],
                                                      in1=rs[:, 0:sz], op0=ALU.mult, op1=ALU.mult),
              reads=[pb, gc, rs], writes=[qn])
        p3 = bank()
        fw.op("pe", lambda e: e.matmul(p3[:, 0:sz], rT[:], qn[:, 0:sz], start=True, stop=True),
              reads=[rT, qn], writes=[p3])
        fw.op("dve", lambda e: e.tensor_tensor(out=t1[:, 0:sz], in0=qn[:, 0:sz], in1=cosb[:, 0:sz], op=ALU.mult),
              reads=[qn, cosb], writes=[t1])
        fw.op("dve", lambda e: e.tensor_tensor(out=t2[:, 0:sz], in0=p3[:, 0:sz], in1=sinb[:, 0:sz], op=ALU.mult),
              reads=[p3, sinb], writes=[t2])
        fw.op("dve", lambda e: e.tensor_tensor(out=dst_ap, in0=t1[:, 0:sz], in1=t2[:, 0:sz], op=ALU.add),
              reads=[t1, t2], writes=[dst_t])

    for l in range(DEPTH):
        last = (l == DEPTH - 1)
        xsrc = [(x_d, 0), (ctx_d, 0)] if l == 0 else [(xs_d, 0), (xs_d, N)]
        qlat = [b for b in lat_blocks if (not last) or b[0] < cfg.NOWN]
        qblocks = qlat + ([] if last else ctx_blocks)
        allblocks = lat_blocks + ctx_blocks

        with nc.allow_non_contiguous_dma(reason="tiny column-layout loads"):
            fw.dma("sp", gcol[:], ng_d[l, :].rearrange("(k p) -> p k", p=128), reads=[ng_d], writes=[gcol])
            fw.dma("sp", bcol[:], bada_d[l, :].rearrange("(k p) -> p k", p=128), reads=[bada_d], writes=[bcol])
            fw.dma("sp", qgc[:], qg_d[l, :].rearrange("(k p) -> p k", p=128), reads=[qg_d], writes=[qgc])
            fw.dma("sp", kgc[:], kg_d[l, :].rearrange("(k p) -> p k", p=128), reads=[kg_d], writes=[kgc])
        fw.op("act", lambda e: e.activation(out=qgc[:], in_=qgc[:], func=AF.Copy, scale=float(128 ** -0.5)),
              reads=[qgc], writes=[qgc])
        pm = bank()
        jobs = []
        for g in range(3 * D // 256):
            def ld(slot, g=g):
                load_w(wada_d, wada_d[l, :, g * 256:(g + 1) * 256], slot)

            def cp(slot, g=g):
                for cb in range(2):
                    ch = g * 2 + cb
                    for k in range(KC):
                        fw.op("pe", lambda e, k=k, cb=cb, ch=ch: e.matmul(
                            pm[:, 2 * ch:2 * ch + 2], slot[:, k, cb * 128:(cb + 1) * 128], scT[:, k, :],
                            start=(k == 0), stop=(k == KC - 1)), reads=[slot, scT], writes=[pm])
            jobs.append((ld, cp))
        run_jobs(jobs)
        for s in range(2):
            fw.op("dve", lambda e, s=s: e.tensor_tensor(out=modc[s][:], in0=pm[:, s:6 * KC:2], in1=bcol[:], op=ALU.add),
                  reads=[pm, bcol], writes=[modc[s]])
            fw.op("dve", lambda e, s=s: e.scalar_tensor_tensor(out=gsc[s][:], in0=modc[s][:, KC:2 * KC], scalar=1.0,
                                                               in1=gcol[:], op0=ALU.add, op1=ALU.mult),
                  reads=[modc[s], gcol], writes=[gsc[s]])

        fw.handoff([hT], [gbl, gbc, fgb, mtt])
        for seg, (src, off) in enumerate(xsrc):
            for ti in range((N if seg == 0 else C) // 128):
                r0 = off + ti * 128
                tok0 = (0 if seg == 0 else N) + ti * 128
                fw.dma("sp", xt[:], src[r0:r0 + 128, :], reads=[src], writes=[xt])
                fw.op("act", lambda e: e.activation(out=junk[:], in_=xt[:], func=AF.Square, accum_out=ssq[:, 0:1]),
                      reads=[xt], writes=[junk, ssq])
                fw.op("act", lambda e: e.activation(out=rstd[:], in_=ssq[:], func=AF.Sqrt, scale=1.0 / D, bias=epsc[:, 0:1]),
                      reads=[ssq, epsc], writes=[rstd])
                fw.op("dve", lambda e: e.reciprocal(out=rstd[:], in_=rstd[:]), reads=[rstd], writes=[rstd])
                fw.op("dve", lambda e: e.tensor_scalar(out=xt[:], in0=xt[:], scalar1=rstd[:, 0:1], scalar2=None, op0=ALU.mult),
                      reads=[xt, rstd], writes=[xt])
                for j0 in range(0, KC, 4):
                    pb = bank()
                    for jj in range(4):
                        j = j0 + jj
                        fw.op("pe", lambda e, j=j, jj=jj, pb=pb: e.transpose(pb[:, jj * 128:(jj + 1) * 128],
                                                                            xt[:, j * 128:(j + 1) * 128], ident[:]),
                              reads=[xt, ident], writes=[pb])
                    for jj in range(4):
                        j = j0 + jj
                        fw.op("act", lambda e, j=j, jj=jj, pb=pb, seg=seg, tok0=tok0: e.activation(
                            out=hT[:, j * NT + tok0:j * NT + tok0 + 128], in_=pb[:, jj * 128:(jj + 1) * 128],
                            func=AF.Identity, scale=gsc[seg][:, j:j + 1], bias=modc[seg][:, j:j + 1]),
                            reads=[pb, gsc[seg], modc[seg]], writes=[hT])

        fw.handoff([KT, Vt, AB], [GT, WO])
        jobs = []
        for g in range((NKV + 1) // 2):
            ncb = min(2, NKV - 2 * g)

            def ld(slot, g=g, ncb=ncb):
                load_w(win_d, win_d[l, :, cfg.OFF_K + g * 256:cfg.OFF_K + g * 256 + ncb * 128], slot)

            def cp(slot, g=g, ncb=ncb):
                for cb in range(ncb):
                    hk = g * 2 + cb
                    for blk in allblocks:
                        s0, sz, seg = blk
                        pb = proj_block(slot, cb, blk)
                        headnorm(pb, sz, kgc, KT[:, KT0 + hk * NT + s0:KT0 + hk * NT + s0 + sz], KT,
                                 rope_blk=(s0 if seg == 0 else None))
            jobs.append((ld, cp))
        for g in range((NKV + 1) // 2):
            ncols = min(2, NKV - 2 * g) * 128

            def ld(slot, g=g, ncols=ncols):
                load_w(win_d, win_d[l, :, cfg.OFF_V + g * 256:cfg.OFF_V + g * 256 + ncols], slot)

            def cp(slot, g=g, ncols=ncols):
                for tt in range(cfg.NTT):
                    pb = bank()
                    for k in range(KC):
                        fw.op("pe", lambda e, k=k, pb=pb, tt=tt: e.matmul(
                            pb[:, 0:ncols], hT[:, k * NT + tt * 128:k * NT + (tt + 1) * 128], slot[:, k, 0:ncols],
                            start=(k == 0), stop=(k == KC - 1)), reads=[slot, hT], writes=[pb])
                    fw.op("act", lambda e, pb=pb, tt=tt: e.activation(
                        out=Vt[:, V0 + tt * cfg.KV_W + g * 256:V0 + tt * cfg.KV_W + g * 256 + ncols],
                        in_=pb[:, 0:ncols], func=AF.Copy), reads=[pb], writes=[Vt])
            jobs.append((ld, cp))
        run_jobs(jobs)

        sz_t, at_t = ftmp["sq"], ftmp["qn"]
        for h0 in range(0, NQ, 2):
            nh = min(2, NQ - h0)
            sq_slot = wslot()
            load_w(win_d, win_d[l, :, cfg.OFF_Q + h0 * 128:cfg.OFF_Q + (h0 + nh) * 128], sq_slot)
            sz_slot = wslot()
            load_w(win_d, win_d[l, :, cfg.OFF_ZA + h0 * 128:cfg.OFF_ZA + (h0 + nh) * 128], sz_slot)
            for hh in range(nh):
                h = h0 + hh
                g = h // 3
                for blk in qblocks:
                    s0, sz, seg = blk
                    pb = proj_block(sq_slot, hh, blk)
                    headnorm(pb, sz, qgc, qT[:, s0:s0 + sz], qT, rope_blk=(s0 if seg == 0 else None))
                for blk in qblocks:
                    s0, sz, seg = blk
                    ktiles = list(range(cfg.NTT)) if seg == 0 else list(range(cfg.NLT, cfg.NTT))
                    po, pd = bank(), bank()
                    for i, kt in enumerate(ktiles):
                        ps_ = bank()
                        while ps_ is po or ps_ is pd:
                            ps_ = bank()
                        fw.op("pe", lambda e, kt=kt, ps_=ps_: e.matmul(
                            ps_[:, 0:sz], KT[:, KT0 + g * NT + kt * 128:KT0 + g * NT + (kt + 1) * 128], qT[:, s0:s0 + sz],
                            start=True, stop=True), reads=[KT, qT], writes=[ps_])
                        p_ = pT[i % 3]
                        fw.op("act", lambda e, p_=p_, ps_=ps_: e.activation(out=p_[:, 0:sz], in_=ps_[:, 0:sz], func=AF.Exp),
                              reads=[ps_], writes=[p_])
                        fw.op("pe", lambda e, kt=kt, p_=p_, i=i: e.matmul(
                            po[:, 0:sz], Vt[:, V0 + kt * cfg.KV_W + g * 128:V0 + kt * cfg.KV_W + (g + 1) * 128], p_[:, 0:sz],
                            start=(i == 0), stop=(i == len(ktiles) - 1)), reads=[Vt, p_], writes=[po])
                        fw.op("pe", lambda e, p_=p_, i=i: e.matmul(
                            pd[:, 0:sz], onesb[:], p_[:, 0:sz],
                            start=(i == 0), stop=(i == len(ktiles) - 1)), reads=[onesb, p_], writes=[pd])
                    rs = ftmp["rs"]
                    fw.op("dve", lambda e: e.reciprocal(out=rs[:, 0:sz], in_=pd[:, 0:sz]), reads=[pd], writes=[rs])
                    fw.op("dve", lambda e: e.tensor_tensor(out=at_t[:, 0:sz], in0=po[:, 0:sz], in1=rs[:, 0:sz], op=ALU.mult),
                          reads=[po, rs], writes=[at_t])
                    pz = proj_block(sz_slot, hh, blk)
                    fw.op("act", lambda e: e.activation(out=sz_t[:, 0:sz], in_=pz[:, 0:sz], func=AF.Silu),
                          reads=[pz], writes=[sz_t])
                    fw.op("dve", lambda e: e.tensor_tensor(out=uT[:, 0:sz], in0=at_t[:, 0:sz], in1=sz_t[:, 0:sz], op=ALU.mult),
                          reads=[at_t, sz_t], writes=[uT])
                    fw.dma("sp", gt_d[h, :, s0:s0 + sz], uT[:, 0:sz], reads=[uT], writes=[gt_d])

        fblocks = allblocks if not last else lat_blocks
        for g0 in range(0, NG, 2):
            ng2 = min(2, NG - g0)
            slot = wslot()
            load_w(win_d, win_d[l, :, cfg.OFF_UB + g0 * 128:cfg.OFF_UB + (g0 + ng2) * 128], slot)
            for gg in range(ng2):
                g = g0 + gg
                for blk in fblocks:
                    s0, sz, seg = blk
                    pb = proj_block(slot, gg, blk)
                    fw.op("act", lambda e: e.activation(out=uT[:, 0:sz], in_=pb[:, 0:sz], func=AF.Copy),
                          reads=[pb], writes=[uT])
                    for t4 in range(0, sz, 128):
                        tt = (s0 + t4) // 128
                        p2 = bank()
                        fw.op("pe", lambda e, t4=t4, p2=p2: e.matmul(p2[:, 0:256], uT[:, t4:t4 + 128], csc[:],
                                                                     start=True, stop=True), reads=[uT, csc], writes=[p2])
                        fw.op("dve", lambda e, tt=tt, p2=p2, g=g: e.tensor_copy(
                            out=AB[:, AB0 + (tt * NG + g) * 256:AB0 + (tt * NG + g + 1) * 256], in_=p2[:, 0:256]),
                            reads=[p2], writes=[AB])
        for blk in qblocks:
            s0, sz, seg = blk
            if seg == 0:
                tab, in_tiles, ot0 = dftl_d, list(range(cfg.NLT)), s0 // 128
            else:
                tab, in_tiles, ot0 = dftc_d, list(range(cfg.NLT, cfg.NTT)), (s0 - N) // 128
            nin = len(in_tiles)
            ybanks = [bank() for _ in range(NG)]
            for oi in range(sz // 128):
                slot = wslot()
                fw.dma("sp", slot[:, 0:nin, :], tab[ot0 + oi, :, :].rearrange("(t p) c -> p t c", p=128),
                       reads=[tab], writes=[slot])
                for g in range(NG):
                    yb = ybanks[g]
                    for ii, it in enumerate(in_tiles):
                        for half in range(2):
                            fw.op("pe", lambda e, g=g, it=it, ii=ii, half=half, yb=yb, oi=oi, slot=slot: e.matmul(
                                yb[:, oi * 128:(oi + 1) * 128],
                                AB[:, AB0 + (it * NG + g) * 256 + half * 128:AB0 + (it * NG + g) * 256 + (half + 1) * 128],
                                slot[:, ii, half * 128:(half + 1) * 128],
                                start=(ii == 0 and half == 0), stop=(ii == nin - 1 and half == 1)),
                                reads=[AB, slot], writes=[yb])
            for g0 in range(0, NG, 2):
                ng2 = min(2, NG - g0)
                slot = wslot()
                load_w(win_d, win_d[l, :, cfg.OFF_ZB + g0 * 128:cfg.OFF_ZB + (g0 + ng2) * 128], slot)
                for gg in range(ng2):
                    g = g0 + gg
                    pz = bank()
                    while any(pz is y for y in ybanks):
                        pz = bank()
                    for k in range(KC):
                        fw.op("pe", lambda e, k=k, pz=pz, gg=gg, slot=slot: e.matmul(
                            pz[:, 0:sz], slot[:, k, gg * 128:(gg + 1) * 128], hT[:, k * NT + s0:k * NT + s0 + sz],
                            start=(k == 0), stop=(k == KC - 1)), reads=[slot, hT], writes=[pz])
                    fw.op("act", lambda e, pz=pz: e.activation(out=sz_t[:, 0:sz], in_=pz[:, 0:sz], func=AF.Silu),
                          reads=[pz], writes=[sz_t])
                    fw.op("dve", lambda e, g=g: e.tensor_tensor(out=uT[:, 0:sz], in0=ybanks[g][:, 0:sz], in1=sz_t[:, 0:sz],
                                                                op=ALU.mult), reads=[ybanks[g], sz_t], writes=[uT])
                    fw.dma("sp", gt_d[NQ + g, :, s0:s0 + sz], uT[:, 0:sz], reads=[uT], writes=[gt_d])

        fw.handoff([GT], [KT, Vt, AB])
        for blk in qblocks:
            s0, sz, seg = blk
            for j in range(KC):
                fw.dma("sp", GT[:, j * NT + s0:j * NT + s0 + sz], gt_d[j, :, s0:s0 + sz], reads=[gt_d], writes=[GT])
        ya_t, yb_t, sg_t = ftmp["t1"], ftmp["t2"], ftmp["sq"]
        for c0 in range(0, D, 256):
            s_p = wslot()
            load_w(wpa_d, wpa_d[l, :, c0:c0 + 256], s_p, k0=0)
            load_w(wpb_d, wpb_d[l, :, c0:c0 + 256], s_p, k0=NQ)
            s_ga = wslot()
            load_w(win_d, win_d[l, :, cfg.OFF_GA + c0:cfg.OFF_GA + c0 + 256], s_ga)
            s_gb = wslot()
            load_w(win_d, win_d[l, :, cfg.OFF_GB + c0:cfg.OFF_GB + c0 + 256], s_gb)
            for cb in range(2):
                j = c0 // 128 + cb
                for blk in qblocks:
                    s0, sz, seg = blk
                    pa = bank()
                    for k in range(NQ):
                        fw.op("pe", lambda e, k=k, pa=pa: e.matmul(pa[:, 0:sz], s_p[:, k, cb * 128:(cb + 1) * 128],
                                                                   GT[:, k * NT + s0:k * NT + s0 + sz],
                                                                   start=(k == 0), stop=(k == NQ - 1)),
                              reads=[s_p, GT], writes=[pa])
                    pbk = bank()
                    for k in range(NG):
                        fw.op("pe", lambda e, k=k, pbk=pbk: e.matmul(pbk[:, 0:sz], s_p[:, NQ + k, cb * 128:(cb + 1) * 128],
                                                                     GT[:, (NQ + k) * NT + s0:(NQ + k) * NT + s0 + sz],
                                                                     start=(k == 0), stop=(k == NG - 1)),
                              reads=[s_p, GT], writes=[pbk])
                    pga = proj_block(s_ga, cb, blk)
                    fw.op("act", lambda e: e.activation(out=sg_t[:, 0:sz], in_=pga[:, 0:sz], func=AF.Sigmoid),
                          reads=[pga], writes=[sg_t])
                    fw.op("dve", lambda e: e.tensor_tensor(out=ya_t[:, 0:sz], in0=pa[:, 0:sz], in1=sg_t[:, 0:sz], op=ALU.mult),
                          reads=[pa, sg_t], writes=[ya_t])
                    pgb = proj_block(s_gb, cb, blk)
                    fw.op("act", lambda e: e.activation(out=sg_t[:, 0:sz], in_=pgb[:, 0:sz], func=AF.Sigmoid),
                          reads=[pgb], writes=[sg_t])
                    fw.op("dve", lambda e: e.tensor_tensor(out=yb_t[:, 0:sz], in0=pbk[:, 0:sz], in1=sg_t[:, 0:sz], op=ALU.mult),
                          reads=[pbk, sg_t], writes=[yb_t])
                    fw.op("dve", lambda e: e.tensor_tensor(out=uT[:, 0:sz], in0=ya_t[:, 0:sz], in1=yb_t[:, 0:sz], op=ALU.add),
                          reads=[ya_t, yb_t], writes=[uT])
                    fw.dma("sp", mt_d[j, :, s0:s0 + sz], uT[:, 0:sz], reads=[uT], writes=[mt_d])

        fw.handoff([WO], [GT])
        fw.handoff([gbl, gbc, fgb, mtt], [hT])
        for k in range(KC):
            fw.dma("pool", WO[:, k * D:(k + 1) * D], wo_d[l, k * 128:(k + 1) * 128, :], reads=[wo_d], writes=[WO])
        for s, (gb_t, goff) in enumerate(((gbl, 0), (gbc, D))):
            if s == 1 and last:
                continue
            for j0 in range(0, KC, 4):
                pg = bank()
                for jj in range(4):
                    j = j0 + jj
                    fw.op("dve", lambda e, s=s, j=j: e.tensor_scalar(out=dg[:], in0=ident[:], scalar1=modc[s][:, 2 * KC + j:2 * KC + j + 1],
                                                                     scalar2=None, op0=ALU.mult),
                          reads=[ident, modc[s]], writes=[dg])
                    fw.op("pe", lambda e, jj=jj, pg=pg: e.matmul(pg[:, jj * 128:(jj + 1) * 128], ones32[:], dg[:],
                                                                 start=True, stop=True), reads=[ones32, dg], writes=[pg])
                fw.op("act", lambda e, pg=pg, gb_t=gb_t, goff=goff, j0=j0: e.activation(
                    out=gb_t[:, goff + j0 * 128:goff + (j0 + 4) * 128], in_=pg[:, 0:512], func=AF.Copy),
                    reads=[pg], writes=[gb_t])
        if last:
            fw.dma("sp", fgb[:, 2 * D:3 * D], fg_d[:].partition_broadcast(128), reads=[fg_d], writes=[fgb])
        tiles6 = [(b0 // 128 + i, seg) for (b0, bsz, seg) in qblocks for i in range(bsz // 128)]
        for (tt, seg) in tiles6:
            gb_t, goff = (gbl, 0) if seg == 0 else (gbc, D)
            src, off = xsrc[seg]
            r0 = off + (tt * 128 - (0 if seg == 0 else N))
            with nc.allow_non_contiguous_dma(reason="m^T tile load (256B runs)"):
                fw.dma("sp", mtt[:, MT0:MT0 + KC * 128].rearrange("p (k t) -> p k t", k=KC) if hasattr(mtt[:, MT0:MT0 + KC * 128], "rearrange") else mtt[:, MT0:MT0 + KC * 128],
                       mt_d[:, :, tt * 128:(tt + 1) * 128].rearrange("k p t -> p k t"), reads=[mt_d], writes=[mtt])
            fw.dma("sp", xt[:], src[r0:r0 + 128, :], reads=[src], writes=[xt])
            t1 = ftmp["t1"]
            for n0 in range(0, D, OB):
                po = bank()
                for k in range(KC):
                    fw.op("pe", lambda e, k=k, po=po, n0=n0: e.matmul(
                        po[:, 0:OB], mtt[:, MT0 + k * 128:MT0 + (k + 1) * 128], WO[:, k * D + n0:k * D + n0 + OB],
                        start=(k == 0), stop=(k == KC - 1)), reads=[mtt, WO], writes=[po])
                fw.op("dve", lambda e, po=po, n0=n0, gb_t=gb_t, goff=goff: e.tensor_tensor(
                    out=t1[:, 0:OB], in0=po[:, 0:OB], in1=gb_t[:, goff + n0:goff + n0 + OB], op=ALU.mult),
                    reads=[po, gb_t], writes=[t1])
                fw.op("dve", lambda e, n0=n0: e.tensor_tensor(out=xt[:, n0:n0 + OB], in0=xt[:, n0:n0 + OB], in1=t1[:, 0:OB],
                                                              op=ALU.add), reads=[xt, t1], writes=[xt])
            if not last:
                fw.dma("sp", xs_d[tt * 128:(tt + 1) * 128, :], xt[:], reads=[xt], writes=[xs_d])
            else:
                fw.op("act", lambda e: e.activation(out=junk[:], in_=xt[:], func=AF.Square, accum_out=ssq[:, 0:1]),
                      reads=[xt], writes=[junk, ssq])
                fw.op("act", lambda e: e.activation(out=rstd[:], in_=ssq[:], func=AF.Sqrt, scale=1.0 / D, bias=epsc[:, 0:1]),
                      reads=[ssq, epsc], writes=[rstd])
                fw.op("dve", lambda e: e.reciprocal(out=rstd[:], in_=rstd[:]), reads=[rstd], writes=[rstd])
                fw.op("dve", lambda e: e.scalar_tensor_tensor(out=xt[:], in0=xt[:], scalar=rstd[:, 0:1], in1=fgb[:, 2 * D:3 * D],
                                                              op0=ALU.mult, op1=ALU.mult), reads=[xt, rstd, fgb], writes=[xt])
                fw.dma("sp", out_d[tt * 128:(tt + 1) * 128, :], xt[:], reads=[xt], writes=[out_d])
    fw.finish([out_d])
    return nc, fw


def make_tables(cfg, perm):
    N, C = cfg.N, cfg.C
    pos = np.asarray(perm, dtype=np.int64)
    row, col = (pos // GRID_W).astype(np.float64), (pos % GRID_W).astype(np.float64)
    inv = ROPE_THETA ** (-np.arange(32, dtype=np.float64) / 32)
    cosT = np.zeros((128, N), np.float32)
    sinT = np.zeros((128, N), np.float32)
    rt = np.zeros((128, 128), np.float32)
    for i in range(128):
        a, part, f = i // 64, (i % 64) // 32, i % 32
        ang = (row if a == 0 else col) * inv[f]
        cosT[i], sinT[i] = np.cos(ang), np.sin(ang)
        if part == 0:
            rt[i + 32, i] = -1.0
        else:
            rt[i - 32, i] = 1.0
    k = np.arange(128)
    angc = 2 * np.pi * np.outer(k, k) / 128
    csc = np.concatenate([np.cos(angc), np.sin(angc)], axis=1) / np.sqrt(128.0)

    def pos_tab(p, n):
        ang = 2 * np.pi * (np.outer(p, p) % n) / n
        cs = np.stack([np.cos(ang), -np.sin(ang)], axis=1) / np.sqrt(float(n))
        cs = cs.reshape(n, 2, n // 128, 128).transpose(2, 0, 1, 3).reshape(n // 128, n, 256)
        return np.ascontiguousarray(cs).astype(ml_dtypes.bfloat16)
    return {"t_cos": cosT, "t_sin": sinT, "t_rt": rt, "t_id": np.eye(128, dtype=np.float32),
            "t_csc": csc.astype(ml_dtypes.bfloat16), "t_dftl": pos_tab(pos, N), "t_dftc": pos_tab(np.arange(C), C)}


_CACHE = {}


def kernel(x, c, ctx, c_ctx, w_ada, b_ada, norm_g, w_in, q_norm_g, k_norm_g,
           w_proj_a, w_proj_b, w_out, final_g):
    x = np.asarray(x, np.float32)
    B, N, D = x.shape
    cfg = Cfg(D=D, N=N, C=ctx.shape[1], TB=512, DEPTH=w_in.shape[0], NOWN=N // 2)
    if "nc" not in _CACHE:
        _CACHE["nc"] = build_program(cfg)[0]
    nc = _CACHE["nc"]
    f = lambda a: np.ascontiguousarray(np.asarray(a, np.float32))
    shared = {"c_ctx": f(c_ctx), "w_ada": f(w_ada), "b_ada": f(b_ada), "norm_g": f(norm_g), "w_in": f(w_in),
              "q_norm_g": f(q_norm_g), "k_norm_g": f(k_norm_g), "w_proj_a": f(w_proj_a), "w_proj_b": f(w_proj_b),
              "w_out": f(w_out), "final_g": f(final_g)}
    perms = [np.arange(N), np.concatenate([np.arange(N // 2, N), np.arange(0, N // 2)])]
    tabs = [make_tables(cfg, p) for p in perms]
    in_maps = []
    for core in range(8):
        b, half = core % 4, core // 4
        m = dict(shared)
        m.update(tabs[half])
        m["x"] = np.ascontiguousarray(x[b][perms[half]])
        m["ctx"] = f(ctx[b])
        m["c"] = f(c[b])
        in_maps.append(m)
    res = run_bass_kernel_spmd(nc, in_maps, core_ids=list(range(8)))
    out = np.empty((B, N, D), np.float32)
    for core in range(8):
        b, half = core % 4, core // 4
        out[b, half * (N // 2):(half + 1) * (N // 2)] = res.results[core]["out"]
    return out
```

```python
import numpy as np
import ml_dtypes
import concourse.bass as bass
import concourse.mybir as mybir
from concourse.bass_utils import run_bass_kernel_spmd

F32 = mybir.dt.float32
BF16 = mybir.dt.bfloat16
AF = mybir.ActivationFunctionType
ALU = mybir.AluOpType
EPS = 1e-6
GRID_W = 64
ROPE_THETA = 10000.0


class T:
    __slots__ = ("t", "lw", "rd", "name")

    def __init__(self, t, name=""):
        self.t = t
        self.lw = {}
        self.rd = []
        self.name = name

    def __getitem__(self, idx):
        return self.t[idx]


class FW:
    NDMA = 6

    def __init__(self, nc):
        self.nc = nc
        self.eng = {"pe": nc.tensor, "act": nc.scalar, "dve": nc.vector,
                    "pool": nc.gpsimd, "sp": nc.sync}
        self.sem = {}
        self.cnt = {}
        for k in self.eng:
            self.sem[k] = nc.alloc_semaphore("s_" + k)
            self.cnt[k] = 0
        self.dsem = {}
        self.dcnt = {}
        self.dnext = {}
        for q in ("sp", "pool"):
            self.dsem[q] = [nc.alloc_semaphore("d_%s%d" % (q, i)) for i in range(self.NDMA)]
            self.dcnt[q] = [0] * self.NDMA
            self.dnext[q] = 0
        self.waited = {}
        self.ninstr = 0

    def sb(self, name, shape, dt):
        return T(self.nc.alloc_sbuf_tensor(name, list(shape), dt), name)

    def ps(self, name, shape, dt=F32):
        return T(self.nc.alloc_psum_tensor(name, list(shape), dt), name)

    def semof(self, key):
        if isinstance(key, tuple):
            return self.dsem[key[0]][key[1]]
        return self.sem[key]

    def _wait(self, e, ev):
        if ev is None:
            return
        key, val = ev
        if self.waited.get((e, key), 0) >= val:
            return
        self.waited[(e, key)] = val
        self.eng[e].wait_ge(self.semof(key), val)

    def _deps(self, e, reads, writes):
        for t in reads:
            for k, v in t.lw.items():
                if k == e and e == "pe":
                    continue
                self._wait(e, (k, v))
        for t in writes:
            for k, v in t.lw.items():
                if k != e or e != "pe":
                    self._wait(e, (k, v))
            for ev in t.rd:
                if ev[0] != e or e != "pe":
                    self._wait(e, ev)

    def _commit(self, ev, reads, writes):
        for t in reads:
            t.rd.append(ev)
            if len(t.rd) > 48:
                m = {}
                for k, v in t.rd:
                    if m.get(k, 0) < v:
                        m[k] = v
                t.rd = list(m.items())
        for t in writes:
            if t.lw.get(ev[0], 0) < ev[1]:
                t.lw[ev[0]] = ev[1]
            t.rd = []

    def op(self, e, fn, reads=(), writes=()):
        self._deps(e, reads, writes)
        ins = fn(self.eng[e])
        self.cnt[e] += 1
        ins.then_inc(self.sem[e], 1)
        ev = (e, self.cnt[e])
        self._commit(ev, reads, writes)
        self.ninstr += 1
        return ev

    def dma(self, q, out, in_, reads=(), writes=(), **kw):
        slot = self.dnext[q]
        self.dnext[q] = (slot + 1) % self.NDMA
        key = (q, slot)
        if self.dcnt[q][slot] > 0:
            self._wait(q, (key, self.dcnt[q][slot]))
        self._deps(q, reads, writes)
        ins = self.eng[q].dma_start(out=out, in_=in_, **kw)
        self.dcnt[q][slot] += 16
        ins.then_inc(self.dsem[q][slot], 16)
        ev = (key, self.dcnt[q][slot])
        self._commit(ev, reads, writes)
        self.ninstr += 1
        return ev

    def handoff(self, new_tiles, old_tiles):
        m = {}
        for t in old_tiles:
            evs = list(t.rd) + list(t.lw.items())
            for k, v in evs:
                if m.get(k, 0) < v:
                    m[k] = v
        for t in new_tiles:
            mm = dict(m)
            for k, v in t.rd + list(t.lw.items()):
                if mm.get(k, 0) < v:
                    mm[k] = v
            t.lw = {}
            t.rd = list(mm.items())

    def finish(self, tiles):
        for t in tiles:
            for k, v in t.lw.items():
                self._wait("sp", (k, v))


class Cfg:
    def __init__(self, D=2048, N=2048, C=256, TB=512, DEPTH=2, NOWN=1024):
        self.D, self.N, self.C, self.TB, self.DEPTH, self.NOWN = D, N, C, TB, DEPTH, NOWN
        self.KC = D // 128
        self.ATT_W = 3 * D // 4
        self.NQ = self.ATT_W // 128
        self.NKV = self.NQ // 3
        self.KV_W = self.NKV * 128
        self.FW_ = D // 4
        self.NG = self.FW_ // 128
        self.OFF_Q = 0
        self.OFF_K = self.ATT_W
        self.OFF_V = self.OFF_K + self.KV_W
        self.OFF_ZA = self.OFF_V + self.KV_W
        self.OFF_UB = self.OFF_ZA + self.ATT_W
        self.OFF_ZB = self.OFF_UB + self.FW_
        self.OFF_GA = self.OFF_ZB + self.FW_
        self.OFF_GB = self.OFF_GA + D
        self.IN_W = self.OFF_GB + D
        self.NT = N + C
        self.NTT = self.NT // 128
        self.NLT = N // 128


def build_program(cfg):
    D, N, C, TB, KC, NT = cfg.D, cfg.N, cfg.C, cfg.TB, cfg.KC, cfg.NT
    NQ, NKV, NG, DEPTH = cfg.NQ, cfg.NKV, cfg.NG, cfg.DEPTH
    CB = min(TB, C)
    nc = bass.Bass("TRN2", target_bir_lowering=False)
    fw = FW(nc)

    def din(name, shape, dt=F32):
        return T(nc.dram_tensor(name, list(shape), dt, kind="ExternalInput").ap(), name)

    def dscr(name, shape, dt):
        return T(nc.dram_tensor(name, list(shape), dt, kind="Internal").ap(), name)

    x_d = din("x", [N, D])
    ctx_d = din("ctx", [C, D])
    c_d = din("c", [D])
    cc_d = din("c_ctx", [D])
    wada_d = din("w_ada", [DEPTH, D, 3 * D])
    bada_d = din("b_ada", [DEPTH, 3 * D])
    ng_d = din("norm_g", [DEPTH, D])
    win_d = din("w_in", [DEPTH, D, cfg.IN_W])
    qg_d = din("q_norm_g", [DEPTH, 128])
    kg_d = din("k_norm_g", [DEPTH, 128])
    wpa_d = din("w_proj_a", [DEPTH, cfg.ATT_W, D])
    wpb_d = din("w_proj_b", [DEPTH, cfg.FW_, D])
    wo_d = din("w_out", [DEPTH, D, D])
    fg_d = din("final_g", [D])
    cos_d = din("t_cos", [128, N])
    sin_d = din("t_sin", [128, N])
    rt_d = din("t_rt", [128, 128])
    id_d = din("t_id", [128, 128])
    csc_d = din("t_csc", [128, 256], BF16)
    dftl_d = din("t_dftl", [N // 128, N, 256], BF16)
    dftc_d = din("t_dftc", [C // 128, C, 256], BF16)
    out_d = T(nc.dram_tensor("out", [cfg.NOWN, D], F32, kind="ExternalOutput").ap(), "out")

    xs_d = dscr("xs_scr", [NT, D], F32)
    gt_d = dscr("gt_scr", [KC, 128, NT], BF16)
    mt_d = dscr("mt_scr", [KC, 128, NT], BF16)

    OB = min(512, TB)
    B1 = max(KC * NT, 6 * D + KC * 128)
    B2 = KC * max(NT, D)
    big1 = nc.alloc_sbuf_tensor("big1", [128, B1], BF16)
    big2 = nc.alloc_sbuf_tensor("big2", [128, B2], BF16)
    big1f = big1.bitcast(F32)
    hT = T(big1, "hT")
    gbl = T(big1f, "gbl")
    gbc = T(big1f, "gbc")
    fgb = T(big1f, "fgb")
    mtt = T(big1, "mtt")
    MT0 = 6 * D
    KT = T(big2, "KT")
    Vt = T(big2, "V")
    AB = T(big2, "AB")
    GT = T(big2, "GT")
    WO = T(big2, "WO")
    KT0, V0, AB0 = 0, NKV * NT, 2 * NKV * NT
    xt = fw.sb("xt", [128, D], F32)
    junk = fw.sb("junk", [128, D], BF16)
    wr = [fw.sb("wr%d" % i, [128, KC, 256], BF16) for i in range(3)]
    ftmp = {n: fw.sb("f_" + n, [128, TB], F32) for n in ("sq", "rs", "qn", "t1", "t2", "cosb", "sinb")}
    qT = fw.sb("qT", [128, NT], BF16)
    pT = [fw.sb("pT%d" % i, [128, TB], BF16) for i in range(4)]
    uT = fw.sb("uT", [128, TB], BF16)
    ident = fw.sb("ident", [128, 128], F32)
    rT = fw.sb("rT", [128, 128], F32)
    dg = fw.sb("dg", [128, 128], F32)
    ones32 = fw.sb("ones32", [128, 128], F32)
    onesb = fw.sb("onesb", [128, 128], BF16)
    csc = fw.sb("csc", [128, 256], BF16)
    epsc = fw.sb("epsc", [128, 1], F32)
    cT = fw.sb("cT", [128, KC, 2], F32)
    scT = fw.sb("scT", [128, KC, 2], BF16)
    gcol = fw.sb("gcol", [128, KC], F32)
    bcol = fw.sb("bcol", [128, 3 * KC], F32)
    modc = [fw.sb("modc%d" % s, [128, 3 * KC], F32) for s in range(2)]
    gsc = [fw.sb("gsc%d" % s, [128, KC], F32) for s in range(2)]
    qgc = fw.sb("qgc", [128, 1], F32)
    kgc = fw.sb("kgc", [128, 1], F32)
    ssq = fw.sb("ssq", [128, 1], F32)
    rstd = fw.sb("rstd", [128, 1], F32)
    banks = [fw.ps("bank%d" % i, [128, 512]) for i in range(8)]
    bstate = {"i": 0}

    def bank():
        b = banks[bstate["i"] % 8]
        bstate["i"] += 1
        return b

    fw.dma("sp", ident[:], id_d[:], reads=[id_d], writes=[ident])
    fw.dma("sp", rT[:], rt_d[:], reads=[rt_d], writes=[rT])
    fw.dma("sp", csc[:], csc_d[:], reads=[csc_d], writes=[csc])
    fw.op("dve", lambda e: e.memset(ones32[:], 1.0), writes=[ones32])
    fw.op("dve", lambda e: e.memset(onesb[:], 1.0), writes=[onesb])
    fw.op("dve", lambda e: e.memset(epsc[:], EPS), writes=[epsc])
    with nc.allow_non_contiguous_dma(reason="tiny column-layout loads"):
        fw.dma("sp", cT[:, :, 0], c_d[:].rearrange("(k p) -> p k", p=128), reads=[c_d], writes=[cT])
        fw.dma("sp", cT[:, :, 1], cc_d[:].rearrange("(k p) -> p k", p=128), reads=[cc_d], writes=[cT])
    fw.op("act", lambda e: e.activation(out=scT[:], in_=cT[:], func=AF.Silu), reads=[cT], writes=[scT])

    lat_blocks = [(s, TB, 0) for s in range(0, N, TB)]
    CB = min(TB, C)
    ctx_blocks = [(N + s, CB, 1) for s in range(0, C, CB)]
    wstate = {"i": 0}

    def wslot():
        w = wr[wstate["i"] % 3]
        wstate["i"] += 1
        return w

    def load_w(dram_t, ap_rows_cols, slot, k0=0):
        rows, cols = ap_rows_cols.shape
        fw.dma("pool", slot[:, k0:k0 + rows // 128, 0:cols],
               ap_rows_cols.rearrange("(t p) c -> p t c", p=128), reads=[dram_t], writes=[slot])

    def run_jobs(jobs):
        slots = [None] * len(jobs)
        if jobs:
            slots[0] = wslot()
            jobs[0][0](slots[0])
        for i, (ld, cp) in enumerate(jobs):
            if i + 1 < len(jobs):
                slots[i + 1] = wslot()
                jobs[i + 1][0](slots[i + 1])
            cp(slots[i])

    def proj_block(slot, cb, blk, kc=KC):
        s0, sz, _ = blk
        pb = bank()
        for k in range(kc):
            fw.op("pe", lambda e, k=k: e.matmul(pb[:, 0:sz], slot[:, k, cb * 128:(cb + 1) * 128],
                                                 hT[:, k * NT + s0:k * NT + s0 + sz],
                                                 start=(k == 0), stop=(k == kc - 1)),
                  reads=[slot, hT], writes=[pb])
        return pb

    def headnorm(pb, sz, gc, dst_ap, dst_t, rope_blk=None):
        sq, rs, qn, t1, t2 = ftmp["sq"], ftmp["rs"], ftmp["qn"], ftmp["t1"], ftmp["t2"]
        fw.op("act", lambda e: e.activation(out=sq[:, 0:sz], in_=pb[:, 0:sz], func=AF.Square), reads=[pb], writes=[sq])
        p2 = bank()
        fw.op("pe", lambda e: e.matmul(p2[:, 0:sz], ones32[:], sq[:, 0:sz], start=True, stop=True),
              reads=[ones32, sq], writes=[p2])
        fw.op("act", lambda e: e.activation(out=rs[:, 0:sz], in_=p2[:, 0:sz], func=AF.Sqrt, scale=1.0 / 128,
                                            bias=epsc[:, 0:1]), reads=[p2, epsc], writes=[rs])
        fw.op("dve", lambda e: e.reciprocal(out=rs[:, 0:sz], in_=rs[:, 0:sz]), reads=[rs], writes=[rs])
        if rope_blk is None:
            fw.op("dve", lambda e: e.scalar_tensor_tensor(out=dst_ap, in0=pb[:, 0:sz], scalar=gc[:, 0:1],
                                                          in1=rs[:, 0:sz], op0=ALU.mult, op1=ALU.mult),
                  reads=[pb, gc, rs], writes=[dst_t])
            return
        s0 = rope_blk
        cosb, sinb = ftmp["cosb"], ftmp["sinb"]
        fw.dma("sp", cosb[:, 0:sz], cos_d[:, s0:s0 + sz], reads=[cos_d], writes=[cosb])
        fw.dma("sp", sinb[:, 0:sz], sin_d[:, s0:s0 + sz], reads=[sin_d], writes=[sinb])
        fw.op("dve", lambda e: e.scalar_tensor_tensor(out=qn[:, 0:sz], in0=pb[:, 0:sz], scalar=gc[:, 0:1],
                                                      in1=rs[:, 0:sz], op0=ALU.mult, op1=ALU.mult),
              reads=[pb, gc, rs], writes=[qn])
        p3 = bank()
        fw.op("pe", lambda e: e.matmul(p3[:, 0:sz], rT[:], qn[:, 0:sz], start=True, stop=True),
              reads=[rT, qn], writes=[p3])
        fw.op("dve", lambda e: e.tensor_tensor(out=t1[:, 0:sz], in0=qn[:, 0:sz], in1=cosb[:, 0:sz], op=ALU.mult),
              reads=[qn, cosb], writes=[t1])
        fw.op("dve", lambda e: e.tensor_tensor(out=t2[:, 0:sz], in0=p3[:, 0:sz], in1=sinb[:, 0:sz], op=ALU.mult),
              reads=[p3, sinb], writes=[t2])
        fw.op("dve", lambda e: e.tensor_tensor(out=dst_ap, in0=t1[:, 0:sz], in1=t2[:, 0:sz], op=ALU.add),
              reads=[t1, t2], writes=[dst_t])

    for l in range(DEPTH):
        last = (l == DEPTH - 1)
        xsrc = [(x_d, 0), (ctx_d, 0)] if l == 0 else [(xs_d, 0), (xs_d, N)]
        qlat = [b for b in lat_blocks if (not last) or b[0] < cfg.NOWN]
        qblocks = qlat + ([] if last else ctx_blocks)
        allblocks = lat_blocks + ctx_blocks

        with nc.allow_non_contiguous_dma(reason="tiny column-layout loads"):
            fw.dma("sp", gcol[:], ng_d[l, :].rearrange("(k p) -> p k", p=128), reads=[ng_d], writes=[gcol])
            fw.dma("sp", bcol[:], bada_d[l, :].rearrange("(k p) -> p k", p=128), reads=[bada_d], writes=[bcol])
            fw.dma("sp", qgc[:], qg_d[l, :].rearrange("(k p) -> p k", p=128), reads=[qg_d], writes=[qgc])
            fw.dma("sp", kgc[:], kg_d[l, :].rearrange("(k p) -> p k", p=128), reads=[kg_d], writes=[kgc])
        fw.op("act", lambda e: e.activation(out=qgc[:], in_=qgc[:], func=AF.Copy, scale=float(128 ** -0.5)),
              reads=[qgc], writes=[qgc])
        pm = bank()
        jobs = []
        for g in range(3 * D // 256):
            def ld(slot, g=g):
                load_w(wada_d, wada_d[l, :, g * 256:(g + 1) * 256], slot)

            def cp(slot, g=g):
                for cb in range(2):
                    ch = g * 2 + cb
                    for k in range(KC):
                        fw.op("pe", lambda e, k=k, cb=cb, ch=ch: e.matmul(
                            pm[:, 2 * ch:2 * ch + 2], slot[:, k, cb * 128:(cb + 1) * 128], scT[:, k, :],
                            start=(k == 0), stop=(k == KC - 1)), reads=[slot, scT], writes=[pm])
            jobs.append((ld, cp))
        run_jobs(jobs)
        for s in range(2):
            fw.op("dve", lambda e, s=s: e.tensor_tensor(out=modc[s][:], in0=pm[:, s:6 * KC:2], in1=bcol[:], op=ALU.add),
                  reads=[pm, bcol], writes=[modc[s]])
            fw.op("dve", lambda e, s=s: e.scalar_tensor_tensor(out=gsc[s][:], in0=modc[s][:, KC:2 * KC], scalar=1.0,
                                                               in1=gcol[:], op0=ALU.add, op1=ALU.mult),
                  reads=[modc[s], gcol], writes=[gsc[s]])

        fw.handoff([hT], [gbl, gbc, fgb, mtt])
        for seg, (src, off) in enumerate(xsrc):
            for ti in range((N if seg == 0 else C) // 128):
                r0 = off + ti * 128
                tok0 = (0 if seg == 0 else N) + ti * 128
                fw.dma("sp", xt[:], src[r0:r0 + 128, :], reads=[src], writes=[xt])
                fw.op("act", lambda e: e.activation(out=junk[:], in_=xt[:], func=AF.Square, accum_out=ssq[:, 0:1]),
                      reads=[xt], writes=[junk, ssq])
                fw.op("act", lambda e: e.activation(out=rstd[:], in_=ssq[:], func=AF.Sqrt, scale=1.0 / D, bias=epsc[:, 0:1]),
                      reads=[ssq, epsc], writes=[rstd])
                fw.op("dve", lambda e: e.reciprocal(out=rstd[:], in_=rstd[:]), reads=[rstd], writes=[rstd])
                fw.op("dve", lambda e: e.tensor_scalar(out=xt[:], in0=xt[:], scalar1=rstd[:, 0:1], scalar2=None, op0=ALU.mult),
                      reads=[xt, rstd], writes=[xt])
                for j0 in range(0, KC, 4):
                    pb = bank()
                    for jj in range(4):
                        j = j0 + jj
                        fw.op("pe", lambda e, j=j, jj=jj, pb=pb: e.transpose(pb[:, jj * 128:(jj + 1) * 128],
                                                                            xt[:, j * 128:(j + 1) * 128], ident[:]),
                              reads=[xt, ident], writes=[pb])
                    for jj in range(4):
                        j = j0 + jj
                        fw.op("act", lambda e, j=j, jj=jj, pb=pb, seg=seg, tok0=tok0: e.activation(
                            out=hT[:, j * NT + tok0:j * NT + tok0 + 128], in_=pb[:, jj * 128:(jj + 1) * 128],
                            func=AF.Identity, scale=gsc[seg][:, j:j + 1], bias=modc[seg][:, j:j + 1]),
                            reads=[pb, gsc[seg], modc[seg]], writes=[hT])

        fw.handoff([KT, Vt, AB], [GT, WO])
        jobs = []
        for g in range((NKV + 1) // 2):
            ncb = min(2, NKV - 2 * g)

            def ld(slot, g=g, ncb=ncb):
                load_w(win_d, win_d[l, :, cfg.OFF_K + g * 256:cfg.OFF_K + g * 256 + ncb * 128], slot)

            def cp(slot, g=g, ncb=ncb):
                for cb in range(ncb):
                    hk = g * 2 + cb
                    pbs = {0: proj_block(slot, cb, allblocks[0])}
                    for bi, blk in enumerate(allblocks):
                        s0, sz, seg = blk
                        if bi + 1 < len(allblocks):
                            pbs[bi + 1] = proj_block(slot, cb, allblocks[bi + 1])
                        headnorm(pbs.pop(bi), sz, kgc, KT[:, KT0 + hk * NT + s0:KT0 + hk * NT + s0 + sz], KT,
                                 rope_blk=(s0 if seg == 0 else None))
            jobs.append((ld, cp))
        for g in range((NKV + 1) // 2):
            ncols = min(2, NKV - 2 * g) * 128

            def ld(slot, g=g, ncols=ncols):
                load_w(win_d, win_d[l, :, cfg.OFF_V + g * 256:cfg.OFF_V + g * 256 + ncols], slot)

            def cp(slot, g=g, ncols=ncols):
                for tt in range(cfg.NTT):
                    pb = bank()
                    for k in range(KC):
                        fw.op("pe", lambda e, k=k, pb=pb, tt=tt: e.matmul(
                            pb[:, 0:ncols], hT[:, k * NT + tt * 128:k * NT + (tt + 1) * 128], slot[:, k, 0:ncols],
                            start=(k == 0), stop=(k == KC - 1)), reads=[slot, hT], writes=[pb])
                    fw.op("act", lambda e, pb=pb, tt=tt: e.activation(
                        out=Vt[:, V0 + tt * cfg.KV_W + g * 256:V0 + tt * cfg.KV_W + g * 256 + ncols],
                        in_=pb[:, 0:ncols], func=AF.Copy), reads=[pb], writes=[Vt])
            jobs.append((ld, cp))
        run_jobs(jobs)

        sz_t, at_t = ftmp["sq"], ftmp["qn"]
        for h0 in range(0, NQ, 2):
            nh = min(2, NQ - h0)
            sq_slot = wslot()
            load_w(win_d, win_d[l, :, cfg.OFF_Q + h0 * 128:cfg.OFF_Q + (h0 + nh) * 128], sq_slot)
            sz_slot = wslot()
            load_w(win_d, win_d[l, :, cfg.OFF_ZA + h0 * 128:cfg.OFF_ZA + (h0 + nh) * 128], sz_slot)
            for hh in range(nh):
                h = h0 + hh
                g = h // 3
                pbs = {0: proj_block(sq_slot, hh, qblocks[0])}
                for bi, blk in enumerate(qblocks):
                    s0, sz, seg = blk
                    if bi + 1 < len(qblocks):
                        pbs[bi + 1] = proj_block(sq_slot, hh, qblocks[bi + 1])
                    headnorm(pbs.pop(bi), sz, qgc, qT[:, s0:s0 + sz], qT, rope_blk=(s0 if seg == 0 else None))
                for blk in qblocks:
                    s0, sz, seg = blk
                    ktiles = list(range(cfg.NTT)) if seg == 0 else list(range(cfg.NLT, cfg.NTT))
                    po, pd = bank(), bank()
                    nkt = len(ktiles)
                    LOOK = 2

                    def s_exp(i):
                        kt = ktiles[i]
                        ps_ = bank()
                        while ps_ is po or ps_ is pd:
                            ps_ = bank()
                        fw.op("pe", lambda e: e.matmul(
                            ps_[:, 0:sz], KT[:, KT0 + g * NT + kt * 128:KT0 + g * NT + (kt + 1) * 128], qT[:, s0:s0 + sz],
                            start=True, stop=True), reads=[KT, qT], writes=[ps_])
                        p_ = pT[i % 4]
                        fw.op("act", lambda e: e.activation(out=p_[:, 0:sz], in_=ps_[:, 0:sz], func=AF.Exp),
                              reads=[ps_], writes=[p_])

                    for i in range(min(LOOK, nkt)):
                        s_exp(i)
                    for i, kt in enumerate(ktiles):
                        if i + LOOK < nkt:
                            s_exp(i + LOOK)
                        p_ = pT[i % 4]
                        fw.op("pe", lambda e, kt=kt, p_=p_, i=i: e.matmul(
                            po[:, 0:sz], Vt[:, V0 + kt * cfg.KV_W + g * 128:V0 + kt * cfg.KV_W + (g + 1) * 128], p_[:, 0:sz],
                            start=(i == 0), stop=(i == nkt - 1)), reads=[Vt, p_], writes=[po])
                        fw.op("pe", lambda e, p_=p_, i=i: e.matmul(
                            pd[:, 0:sz], onesb[:], p_[:, 0:sz],
                            start=(i == 0), stop=(i == nkt - 1)), reads=[onesb, p_], writes=[pd])
                    rs = ftmp["rs"]
                    fw.op("dve", lambda e: e.reciprocal(out=rs[:, 0:sz], in_=pd[:, 0:sz]), reads=[pd], writes=[rs])
                    fw.op("dve", lambda e: e.tensor_tensor(out=at_t[:, 0:sz], in0=po[:, 0:sz], in1=rs[:, 0:sz], op=ALU.mult),
                          reads=[po, rs], writes=[at_t])
                    pz = proj_block(sz_slot, hh, blk)
                    fw.op("act", lambda e: e.activation(out=sz_t[:, 0:sz], in_=pz[:, 0:sz], func=AF.Silu),
                          reads=[pz], writes=[sz_t])
                    fw.op("dve", lambda e: e.tensor_tensor(out=uT[:, 0:sz], in0=at_t[:, 0:sz], in1=sz_t[:, 0:sz], op=ALU.mult),
                          reads=[at_t, sz_t], writes=[uT])
                    fw.dma("sp", gt_d[h, :, s0:s0 + sz], uT[:, 0:sz], reads=[uT], writes=[gt_d])

        fblocks = allblocks if not last else lat_blocks
        for g0 in range(0, NG, 2):
            ng2 = min(2, NG - g0)
            slot = wslot()
            load_w(win_d, win_d[l, :, cfg.OFF_UB + g0 * 128:cfg.OFF_UB + (g0 + ng2) * 128], slot)
            for gg in range(ng2):
                g = g0 + gg
                for blk in fblocks:
                    s0, sz, seg = blk
                    pb = proj_block(slot, gg, blk)
                    fw.op("act", lambda e: e.activation(out=uT[:, 0:sz], in_=pb[:, 0:sz], func=AF.Copy),
                          reads=[pb], writes=[uT])
                    for t4 in range(0, sz, 128):
                        tt = (s0 + t4) // 128
                        p2 = bank()
                        fw.op("pe", lambda e, t4=t4, p2=p2: e.matmul(p2[:, 0:256], uT[:, t4:t4 + 128], csc[:],
                                                                     start=True, stop=True), reads=[uT, csc], writes=[p2])
                        fw.op("dve", lambda e, tt=tt, p2=p2, g=g: e.tensor_copy(
                            out=AB[:, AB0 + (tt * NG + g) * 256:AB0 + (tt * NG + g + 1) * 256], in_=p2[:, 0:256]),
                            reads=[p2], writes=[AB])
        for blk in qblocks:
            s0, sz, seg = blk
            if seg == 0:
                tab, in_tiles, ot0 = dftl_d, list(range(cfg.NLT)), s0 // 128
            else:
                tab, in_tiles, ot0 = dftc_d, list(range(cfg.NLT, cfg.NTT)), (s0 - N) // 128
            nin = len(in_tiles)
            ybanks = [bank() for _ in range(NG)]
            for oi in range(sz // 128):
                slot = wslot()
                fw.dma("sp", slot[:, 0:nin, :], tab[ot0 + oi, :, :].rearrange("(t p) c -> p t c", p=128),
                       reads=[tab], writes=[slot])
                for g in range(NG):
                    yb = ybanks[g]
                    for ii, it in enumerate(in_tiles):
                        for half in range(2):
                            fw.op("pe", lambda e, g=g, it=it, ii=ii, half=half, yb=yb, oi=oi, slot=slot: e.matmul(
                                yb[:, oi * 128:(oi + 1) * 128],
                                AB[:, AB0 + (it * NG + g) * 256 + half * 128:AB0 + (it * NG + g) * 256 + (half + 1) * 128],
                                slot[:, ii, half * 128:(half + 1) * 128],
                                start=(ii == 0 and half == 0), stop=(ii == nin - 1 and half == 1)),
                                reads=[AB, slot], writes=[yb])
            for g0 in range(0, NG, 2):
                ng2 = min(2, NG - g0)
                slot = wslot()
                load_w(win_d, win_d[l, :, cfg.OFF_ZB + g0 * 128:cfg.OFF_ZB + (g0 + ng2) * 128], slot)
                for gg in range(ng2):
                    g = g0 + gg
                    pz = bank()
                    while any(pz is y for y in ybanks):
                        pz = bank()
                    for k in range(KC):
                        fw.op("pe", lambda e, k=k, pz=pz, gg=gg, slot=slot: e.matmul(
                            pz[:, 0:sz], slot[:, k, gg * 128:(gg + 1) * 128], hT[:, k * NT + s0:k * NT + s0 + sz],
                            start=(k == 0), stop=(k == KC - 1)), reads=[slot, hT], writes=[pz])
                    fw.op("act", lambda e, pz=pz: e.activation(out=sz_t[:, 0:sz], in_=pz[:, 0:sz], func=AF.Silu),
                          reads=[pz], writes=[sz_t])
                    fw.op("dve", lambda e, g=g: e.tensor_tensor(out=uT[:, 0:sz], in0=ybanks[g][:, 0:sz], in1=sz_t[:, 0:sz],
                                                                op=ALU.mult), reads=[ybanks[g], sz_t], writes=[uT])
                    fw.dma("sp", gt_d[NQ + g, :, s0:s0 + sz], uT[:, 0:sz], reads=[uT], writes=[gt_d])

        fw.handoff([GT], [KT, Vt, AB])
        for blk in qblocks:
            s0, sz, seg = blk
            for j in range(KC):
                fw.dma("sp", GT[:, j * NT + s0:j * NT + s0 + sz], gt_d[j, :, s0:s0 + sz], reads=[gt_d], writes=[GT])
        ya_t, yb_t, sg_t = ftmp["t1"], ftmp["t2"], ftmp["sq"]
        for c0 in range(0, D, 256):
            s_p = wslot()
            load_w(wpa_d, wpa_d[l, :, c0:c0 + 256], s_p, k0=0)
            load_w(wpb_d, wpb_d[l, :, c0:c0 + 256], s_p, k0=NQ)
            s_ga = wslot()
            load_w(win_d, win_d[l, :, cfg.OFF_GA + c0:cfg.OFF_GA + c0 + 256], s_ga)
            s_gb = wslot()
            load_w(win_d, win_d[l, :, cfg.OFF_GB + c0:cfg.OFF_GB + c0 + 256], s_gb)
            for cb in range(2):
                j = c0 // 128 + cb
                for blk in qblocks:
                    s0, sz, seg = blk
                    pa = bank()
                    for k in range(NQ):
                        fw.op("pe", lambda e, k=k, pa=pa: e.matmul(pa[:, 0:sz], s_p[:, k, cb * 128:(cb + 1) * 128],
                                                                   GT[:, k * NT + s0:k * NT + s0 + sz],
                                                                   start=(k == 0), stop=(k == NQ - 1)),
                              reads=[s_p, GT], writes=[pa])
                    pbk = bank()
                    for k in range(NG):
                        fw.op("pe", lambda e, k=k, pbk=pbk: e.matmul(pbk[:, 0:sz], s_p[:, NQ + k, cb * 128:(cb + 1) * 128],
                                                                     GT[:, (NQ + k) * NT + s0:(NQ + k) * NT + s0 + sz],
                                                                     start=(k == 0), stop=(k == NG - 1)),
                              reads=[s_p, GT], writes=[pbk])
                    pga = proj_block(s_ga, cb, blk)
                    fw.op("act", lambda e: e.activation(out=sg_t[:, 0:sz], in_=pga[:, 0:sz], func=AF.Sigmoid),
                          reads=[pga], writes=[sg_t])
                    fw.op("dve", lambda e: e.tensor_tensor(out=ya_t[:, 0:sz], in0=pa[:, 0:sz], in1=sg_t[:, 0:sz], op=ALU.mult),
                          reads=[pa, sg_t], writes=[ya_t])
                    pgb = proj_block(s_gb, cb, blk)
                    fw.op("act", lambda e: e.activation(out=sg_t[:, 0:sz], in_=pgb[:, 0:sz], func=AF.Sigmoid),
                          reads=[pgb], writes=[sg_t])
                    fw.op("dve", lambda e: e.tensor_tensor(out=yb_t[:, 0:sz], in0=pbk[:, 0:sz], in1=sg_t[:, 0:sz], op=ALU.mult),
                          reads=[pbk, sg_t], writes=[yb_t])
                    fw.op("dve", lambda e: e.tensor_tensor(out=uT[:, 0:sz], in0=ya_t[:, 0:sz], in1=yb_t[:, 0:sz], op=ALU.add),
                          reads=[ya_t, yb_t], writes=[uT])
                    fw.dma("sp", mt_d[j, :, s0:s0 + sz], uT[:, 0:sz], reads=[uT], writes=[mt_d])

        fw.handoff([WO], [GT])
        fw.handoff([gbl, gbc, fgb, mtt], [hT])
        for k in range(KC):
            fw.dma("pool", WO[:, k * D:(k + 1) * D], wo_d[l, k * 128:(k + 1) * 128, :], reads=[wo_d], writes=[WO])
        for s, (gb_t, goff) in enumerate(((gbl, 0), (gbc, D))):
            if s == 1 and last:
                continue
            for j0 in range(0, KC, 4):
                pg = bank()
                for jj in range(4):
                    j = j0 + jj
                    fw.op("dve", lambda e, s=s, j=j: e.tensor_scalar(out=dg[:], in0=ident[:], scalar1=modc[s][:, 2 * KC + j:2 * KC + j + 1],
                                                                     scalar2=None, op0=ALU.mult),
                          reads=[ident, modc[s]], writes=[dg])
                    fw.op("pe", lambda e, jj=jj, pg=pg: e.matmul(pg[:, jj * 128:(jj + 1) * 128], ones32[:], dg[:],
                                                                 start=True, stop=True), reads=[ones32, dg], writes=[pg])
                fw.op("act", lambda e, pg=pg, gb_t=gb_t, goff=goff, j0=j0: e.activation(
                    out=gb_t[:, goff + j0 * 128:goff + (j0 + 4) * 128], in_=pg[:, 0:512], func=AF.Copy),
                    reads=[pg], writes=[gb_t])
        if last:
            fw.dma("sp", fgb[:, 2 * D:3 * D], fg_d[:].partition_broadcast(128), reads=[fg_d], writes=[fgb])
        tiles6 = [(b0 // 128 + i, seg) for (b0, bsz, seg) in qblocks for i in range(bsz // 128)]
        for (tt, seg) in tiles6:
            gb_t, goff = (gbl, 0) if seg == 0 else (gbc, D)
            src, off = xsrc[seg]
            r0 = off + (tt * 128 - (0 if seg == 0 else N))
            with nc.allow_non_contiguous_dma(reason="m^T tile load (256B runs)"):
                fw.dma("sp", mtt[:, MT0:MT0 + KC * 128].rearrange("p (k t) -> p k t", k=KC) if hasattr(mtt[:, MT0:MT0 + KC * 128], "rearrange") else mtt[:, MT0:MT0 + KC * 128],
                       mt_d[:, :, tt * 128:(tt + 1) * 128].rearrange("k p t -> p k t"), reads=[mt_d], writes=[mtt])
            fw.dma("sp", xt[:], src[r0:r0 + 128, :], reads=[src], writes=[xt])
            t1 = ftmp["t1"]
            for n0 in range(0, D, OB):
                po = bank()
                for k in range(KC):
                    fw.op("pe", lambda e, k=k, po=po, n0=n0: e.matmul(
                        po[:, 0:OB], mtt[:, MT0 + k * 128:MT0 + (k + 1) * 128], WO[:, k * D + n0:k * D + n0 + OB],
                        start=(k == 0), stop=(k == KC - 1)), reads=[mtt, WO], writes=[po])
                fw.op("dve", lambda e, po=po, n0=n0, gb_t=gb_t, goff=goff: e.tensor_tensor(
                    out=t1[:, 0:OB], in0=po[:, 0:OB], in1=gb_t[:, goff + n0:goff + n0 + OB], op=ALU.mult),
                    reads=[po, gb_t], writes=[t1])
                fw.op("dve", lambda e, n0=n0: e.tensor_tensor(out=xt[:, n0:n0 + OB], in0=xt[:, n0:n0 + OB], in1=t1[:, 0:OB],
                                                              op=ALU.add), reads=[xt, t1], writes=[xt])
            if not last:
                fw.dma("sp", xs_d[tt * 128:(tt + 1) * 128, :], xt[:], reads=[xt], writes=[xs_d])
            else:
                fw.op("act", lambda e: e.activation(out=junk[:], in_=xt[:], func=AF.Square, accum_out=ssq[:, 0:1]),
                      reads=[xt], writes=[junk, ssq])
                fw.op("act", lambda e: e.activation(out=rstd[:], in_=ssq[:], func=AF.Sqrt, scale=1.0 / D, bias=epsc[:, 0:1]),
                      reads=[ssq, epsc], writes=[rstd])
                fw.op("dve", lambda e: e.reciprocal(out=rstd[:], in_=rstd[:]), reads=[rstd], writes=[rstd])
                fw.op("dve", lambda e: e.scalar_tensor_tensor(out=xt[:], in0=xt[:], scalar=rstd[:, 0:1], in1=fgb[:, 2 * D:3 * D],
                                                              op0=ALU.mult, op1=ALU.mult), reads=[xt, rstd, fgb], writes=[xt])
                fw.dma("sp", out_d[tt * 128:(tt + 1) * 128, :], xt[:], reads=[xt], writes=[out_d])
    fw.finish([out_d])
    return nc, fw


def make_tables(cfg, perm):
    N, C = cfg.N, cfg.C
    pos = np.asarray(perm, dtype=np.int64)
    row, col = (pos // GRID_W).astype(np.float64), (pos % GRID_W).astype(np.float64)
    inv = ROPE_THETA ** (-np.arange(32, dtype=np.float64) / 32)
    cosT = np.zeros((128, N), np.float32)
    sinT = np.zeros((128, N), np.float32)
    rt = np.zeros((128, 128), np.float32)
    for i in range(128):
        a, part, f = i // 64, (i % 64) // 32, i % 32
        ang = (row if a == 0 else col) * inv[f]
        cosT[i], sinT[i] = np.cos(ang), np.sin(ang)
        if part == 0:
            rt[i + 32, i] = -1.0
        else:
            rt[i - 32, i] = 1.0
    k = np.arange(128)
    angc = 2 * np.pi * np.outer(k, k) / 128
    csc = np.concatenate([np.cos(angc), np.sin(angc)], axis=1) / np.sqrt(128.0)

    def pos_tab(p, n):
        ang = 2 * np.pi * (np.outer(p, p) % n) / n
        cs = np.stack([np.cos(ang), -np.sin(ang)], axis=1) / np.sqrt(float(n))
        cs = cs.reshape(n, 2, n // 128, 128).transpose(2, 0, 1, 3).reshape(n // 128, n, 256)
        return np.ascontiguousarray(cs).astype(ml_dtypes.bfloat16)
    return {"t_cos": cosT, "t_sin": sinT, "t_rt": rt, "t_id": np.eye(128, dtype=np.float32),
            "t_csc": csc.astype(ml_dtypes.bfloat16), "t_dftl": pos_tab(pos, N), "t_dftc": pos_tab(np.arange(C), C)}


_CACHE = {}


def kernel(x, c, ctx, c_ctx, w_ada, b_ada, norm_g, w_in, q_norm_g, k_norm_g,
           w_proj_a, w_proj_b, w_out, final_g):
    x = np.asarray(x, np.float32)
    B, N, D = x.shape
    cfg = Cfg(D=D, N=N, C=ctx.shape[1], TB=512, DEPTH=w_in.shape[0], NOWN=N // 2)
    if "nc" not in _CACHE:
        _CACHE["nc"] = build_program(cfg)[0]
    nc = _CACHE["nc"]
    f = lambda a: np.ascontiguousarray(np.asarray(a, np.float32))
    shared = {"c_ctx": f(c_ctx), "w_ada": f(w_ada), "b_ada": f(b_ada), "norm_g": f(norm_g), "w_in": f(w_in),
              "q_norm_g": f(q_norm_g), "k_norm_g": f(k_norm_g), "w_proj_a": f(w_proj_a), "w_proj_b": f(w_proj_b),
              "w_out": f(w_out), "final_g": f(final_g)}
    perms = [np.arange(N), np.concatenate([np.arange(N // 2, N), np.arange(0, N // 2)])]
    tabs = [make_tables(cfg, p) for p in perms]
    in_maps = []
    for core in range(8):
        b, half = core % 4, core // 4
        m = dict(shared)
        m.update(tabs[half])
        m["x"] = np.ascontiguousarray(x[b][perms[half]])
        m["ctx"] = f(ctx[b])
        m["c"] = f(c[b])
        in_maps.append(m)
    res = run_bass_kernel_spmd(nc, in_maps, core_ids=list(range(8)))
    out = np.empty((B, N, D), np.float32)
    for core in range(8):
        b, half = core % 4, core // 4
        out[b, half * (N // 2):(half + 1) * (N // 2)] = res.results[core]["out"]
    return out
```
